# Optimizing a Trainium2 kernel written in Bass

```python
import math
import jax, jax.numpy as jnp
from jax import lax
import numpy as np

D_MODEL = 2048
BATCH = 4
SEQ = 4096
DEPTH = 2

D_MIX = D_MODEL
HEAD_DIM = 64
N_Q_HEADS = 16
N_KV_HEADS = 4
GQ = N_Q_HEADS // N_KV_HEADS
WINDOW = 128
ATTN_BLOCK = 128
ROT_DIM = HEAD_DIM // 4
ROPE_THETA = 500000.0
CONV_GROUPS = 8
CONV_CH = CONV_GROUPS * HEAD_DIM
CONV_WIDTH = 31
SGU_HEADS = 8
SGU_CH = SGU_HEADS * HEAD_DIM
SGU_CHUNK = 128
D_FF = 5632
FFN_RESIDUAL_WEIGHT = 0.5
NORM_EPS = 1e-5

Q_END = N_Q_HEADS * HEAD_DIM
K_END = Q_END + N_KV_HEADS * HEAD_DIM
V_END = K_END + N_KV_HEADS * HEAD_DIM
CONV_END = V_END + 2 * CONV_CH
IN_COLS = CONV_END + 2 * SGU_CH

kernel_name = 'hybrid_parallel_conv_sgu_swa_block'


def rms_norm(x, g):
    xf = x.astype(jnp.float32)
    y = xf * lax.rsqrt(jnp.mean(xf * xf, axis=-1, keepdims=True) + NORM_EPS)
    return (y * g.astype(jnp.float32)).astype(x.dtype)


def layer_norm(x, g, b):
    xf = x.astype(jnp.float32)
    mu = jnp.mean(xf, axis=-1, keepdims=True)
    xc = xf - mu
    y = xc * lax.rsqrt(jnp.mean(xc * xc, axis=-1, keepdims=True) + NORM_EPS)
    return (y * g.astype(jnp.float32) + b.astype(jnp.float32)).astype(x.dtype)


def swiglu(h, w_in, w_out):
    gu = h @ w_in
    return (jax.nn.silu(gu[..., :D_FF]) * gu[..., D_FF:]) @ w_out


def rope_tables(positions):
    inv_freq = 1.0 / (ROPE_THETA ** (jnp.arange(0, ROT_DIM, 2, dtype=jnp.float32) / ROT_DIM))
    ang = positions.astype(jnp.float32)[..., None] * inv_freq
    return jnp.cos(ang), jnp.sin(ang)


def apply_partial_rope(t, cos, sin):
    half = ROT_DIM // 2
    t1 = t[..., :half].astype(jnp.float32)
    t2 = t[..., half:ROT_DIM].astype(jnp.float32)
    c = cos[:, :, None, :]
    s = sin[:, :, None, :]
    rot = jnp.concatenate([t1 * c - t2 * s, t2 * c + t1 * s], axis=-1).astype(t.dtype)
    return jnp.concatenate([rot, t[..., ROT_DIM:]], axis=-1)


def sliding_window_attention(q, k, v, sinks):
    B, S = q.shape[0], q.shape[1]
    nb = S // ATTN_BLOCK
    qb = q.reshape(B, nb, ATTN_BLOCK, N_KV_HEADS, GQ, HEAD_DIM)

    def with_prev(t):
        tb = t.reshape(B, nb, ATTN_BLOCK, N_KV_HEADS, HEAD_DIM)
        prev = jnp.concatenate([jnp.zeros_like(tb[:, :1]), tb[:, :-1]], axis=1)
        return jnp.concatenate([prev, tb], axis=2)

    kk, vv = with_prev(k), with_prev(v)
    scores = jnp.einsum('bnqhgd,bnkhd->bnhgqk', qb, kk).astype(jnp.float32) * (HEAD_DIM ** -0.5)
    qi = jnp.arange(ATTN_BLOCK)[:, None]
    kj = jnp.arange(2 * ATTN_BLOCK)[None, :]
    dist = qi + ATTN_BLOCK - kj
    band = (dist >= 0) & (dist < WINDOW)
    kpos = jnp.arange(nb)[:, None, None] * ATTN_BLOCK + kj[None] - ATTN_BLOCK
    mask = band[None] & (kpos >= 0)
    scores = jnp.where(mask[None, :, None, None], scores, jnp.float32(-1e30))
    s = sinks.astype(jnp.float32).reshape(N_KV_HEADS, GQ)[None, None, :, :, None, None]
    m = jnp.maximum(jnp.max(scores, axis=-1, keepdims=True), s)
    p = jnp.exp(scores - m)
    p = p / (jnp.sum(p, axis=-1, keepdims=True) + jnp.exp(s - m))
    out = jnp.einsum('bnhgqk,bnkhd->bnqhgd', p.astype(v.dtype), vv)
    return out.reshape(B, S, N_Q_HEADS * HEAD_DIM)


def conv_module(a, dw_w, dw_b, ln_g, ln_b):
    h = a[..., :CONV_CH] * jax.nn.sigmoid(a[..., CONV_CH:])
    hp = jnp.pad(h, ((0, 0), (CONV_WIDTH - 1, 0), (0, 0)))
    y = lax.conv_general_dilated(hp, dw_w[:, None, :].astype(h.dtype), window_strides=(1,),
                                 padding='VALID', dimension_numbers=('NWC', 'WIO', 'NWC'),
                                 feature_group_count=CONV_CH) + dw_b
    return jax.nn.silu(layer_norm(y, ln_g, ln_b))


def spatial_gating(a, ln_g, ln_b, w_s, b_s):
    B, S = a.shape[0], a.shape[1]
    u = a[..., :SGU_CH]
    v = layer_norm(a[..., SGU_CH:], ln_g, ln_b)
    vb = v.reshape(B, S // SGU_CHUNK, SGU_CHUNK, SGU_HEADS, HEAD_DIM)
    causal = jnp.tril(jnp.ones((SGU_CHUNK, SGU_CHUNK), dtype=bool))
    ws = jnp.where(causal[None], w_s, jnp.zeros_like(w_s))
    mixed = jnp.einsum('hts,bnshd->bnthd', ws, vb) + b_s.T[None, None, :, :, None]
    return u * mixed.reshape(B, S, SGU_CH)


def setup_inputs(seed: int = 0) -> dict:
    key = jax.random.key(seed)
    ks = jax.random.split(key, 24)
    L, D, F = DEPTH, D_MODEL, D_FF
    nrm = lambda k, shape, scale: jax.random.normal(k, shape, jnp.float32) * scale
    gain = lambda k, shape: 1.0 + 0.05 * jax.random.normal(k, shape, jnp.float32)
    x = jax.random.normal(ks[0], (BATCH, SEQ, D), jnp.float32)
    offs = jax.random.randint(ks[1], (BATCH, 1), 0, 1024, dtype=jnp.int32)
    positions = (jnp.arange(SEQ, dtype=jnp.int32)[None, :] + offs).astype(jnp.int32)
    return {
        'x': x,
        'positions': positions,
        'norm_ffn1': gain(ks[2], (L, D)),
        'ffn1_w_in': nrm(ks[3], (L, D, 2 * F), D ** -0.5),
        'ffn1_w_out': nrm(ks[4], (L, F, D), F ** -0.5),
        'norm_mix': gain(ks[5], (L, D)),
        'w_in': nrm(ks[6], (L, D, IN_COLS), D ** -0.5),
        'conv_dw_w': nrm(ks[7], (L, CONV_WIDTH, CONV_CH), CONV_WIDTH ** -0.5),
        'conv_dw_b': nrm(ks[8], (L, CONV_CH), 0.02),
        'conv_ln_g': gain(ks[9], (L, CONV_CH)),
        'conv_ln_b': nrm(ks[10], (L, CONV_CH), 0.02),
        'sgu_ln_g': gain(ks[11], (L, SGU_CH)),
        'sgu_ln_b': nrm(ks[12], (L, SGU_CH), 0.02),
        'sgu_w': nrm(ks[13], (L, SGU_HEADS, SGU_CHUNK, SGU_CHUNK), SGU_CHUNK ** -0.5),
        'sgu_b': 1.0 + 0.1 * jax.random.normal(ks[14], (L, SGU_HEADS, SGU_CHUNK), jnp.float32),
        'attn_sinks': nrm(ks[15], (L, N_Q_HEADS), 0.5),
        'w_out': nrm(ks[16], (L, D_MIX, D), D_MIX ** -0.5),
        'norm_ffn2': gain(ks[17], (L, D)),
        'ffn2_w_in': nrm(ks[18], (L, D, 2 * F), D ** -0.5),
        'ffn2_w_out': nrm(ks[19], (L, F, D), F ** -0.5),
        'final_norm': gain(ks[20], (D,)),
    }


def reference(x, positions, norm_ffn1, ffn1_w_in, ffn1_w_out, norm_mix, w_in, conv_dw_w, conv_dw_b,
              conv_ln_g, conv_ln_b, sgu_ln_g, sgu_ln_b, sgu_w, sgu_b, attn_sinks, w_out,
              norm_ffn2, ffn2_w_in, ffn2_w_out, final_norm):
    B, S = x.shape[0], x.shape[1]
    cos, sin = rope_tables(positions)
    for l in range(DEPTH):
        h = rms_norm(x, norm_ffn1[l])
        x = x + FFN_RESIDUAL_WEIGHT * swiglu(h, ffn1_w_in[l], ffn1_w_out[l])
        h = rms_norm(x, norm_mix[l])
        p = h @ w_in[l]
        q = apply_partial_rope(p[..., :Q_END].reshape(B, S, N_Q_HEADS, HEAD_DIM), cos, sin)
        k = apply_partial_rope(p[..., Q_END:K_END].reshape(B, S, N_KV_HEADS, HEAD_DIM), cos, sin)
        v = p[..., K_END:V_END].reshape(B, S, N_KV_HEADS, HEAD_DIM)
        attn = sliding_window_attention(q, k, v, attn_sinks[l])
        conv = conv_module(p[..., V_END:CONV_END], conv_dw_w[l], conv_dw_b[l],
                           conv_ln_g[l], conv_ln_b[l])
        sgu = spatial_gating(p[..., CONV_END:], sgu_ln_g[l], sgu_ln_b[l],
                             sgu_w[l], sgu_b[l])
        x = x + jnp.concatenate([attn, conv, sgu], axis=-1) @ w_out[l]
        h = rms_norm(x, norm_ffn2[l])
        x = x + FFN_RESIDUAL_WEIGHT * swiglu(h, ffn2_w_in[l], ffn2_w_out[l])
    return rms_norm(x, final_norm)
```

```python
import contextlib
import numpy as np
import concourse.bass as bass
import concourse.mybir as mybir
from concourse.bass_utils import run_bass_kernel_spmd

dt = mybir.dt
F32, BF16, F32R, I32 = dt.float32, dt.bfloat16, dt.float32r, dt.int32
ALU = mybir.AluOpType
AF = mybir.ActivationFunctionType
ESZ = {F32: 4, F32R: 4, BF16: 2, I32: 4}

P = 128
D = 2048
DC = 16
FF = 5632
FC = 44
FPARTS = [(0, 8), (8, 16), (16, 24), (24, 32), (32, 40), (40, 44)]
INC = 3584
EPS = 1e-5
NSLOT = 4
SLOTF = 2048
SAME_SYNC = True

L_G1, L_GM, L_G2 = 0, 16, 32
L_CW = 48
L_CB = L_CW + 124
L_CLG = L_CB + 4
L_CLB = L_CLG + 4
L_SINK = L_CLB + 4
L_BST = L_SINK + 16
L_SG = L_BST + 512
L_SB = L_SG + 512
L_SIZE = L_SB + 512
G_BASE = 2 * L_SIZE
G_FN = G_BASE
G_IFC = G_FN + 16
G_IFS = G_IFC + 1
G_VALID = G_IFS + 1
G_MCUR = G_VALID + 1
G_MPREV = G_MCUR + 128
G_PERM = G_MPREV + 128
G_EPS = G_PERM + 128
G_SWAP = G_EPS + 1
G_MLO = G_SWAP + 128
G_MHI = G_MLO + 1
NCST = G_MHI + 1


def ap_range(ap):
    es = ESZ[ap.dtype]
    pat = ap.ap
    pstride = pat[0][0]
    off = ap.offset % pstride if pstride > 0 else ap.offset
    ext = 1
    for st, cnt in pat[1:]:
        ext += (cnt - 1) * abs(st)
    return off * es, (off + ext) * es


class IMap:
    def __init__(self):
        self.segs = []

    def access(self, lo, hi, idx, key, write):
        deps = set()
        res = []
        covered = []
        for s in self.segs:
            if s[1] <= lo or s[0] >= hi:
                res.append(s)
                continue
            if s[0] < lo:
                res.append([s[0], lo, s[2], dict(s[3])])
            if s[1] > hi:
                res.append([hi, s[1], s[2], dict(s[3])])
            a, b = max(s[0], lo), min(s[1], hi)
            if s[2] is not None:
                deps.add(s[2])
            if write:
                deps.update(s[3].values())
            else:
                r = dict(s[3])
                r[key] = idx
                res.append([a, b, s[2], r])
                covered.append((a, b))
        if write:
            res.append([lo, hi, idx, {}])
        else:
            covered.sort()
            cur = lo
            for a, b in covered:
                if a > cur:
                    res.append([cur, a, None, {key: idx}])
                cur = max(cur, b)
            if cur < hi:
                res.append([cur, hi, None, {key: idx}])
        res.sort(key=lambda s: s[0])
        self.segs = res
        return deps


class Sched:
    ENGS = ("pe", "act", "dve", "pool", "sp")

    def __init__(self, nc):
        self.nc = nc
        self.ops = []
        self.maps = {}
        self.collect = False
        self.banks = []
        self.bank_i = 0
        self.dma_counts = {}
        self.last_pe = None
        self.pe_mode = "full"

    def bank(self):
        b = self.banks[self.bank_i % len(self.banks)]
        self.bank_i += 1
        return b

    def add(self, eng, fn, reads=(), writes=(), dkey=None, force=()):
        if self.collect:
            return
        idx = len(self.ops)
        key = eng if dkey is None else ("d", idx)
        deps = set()
        for ap in reads:
            lo, hi = ap_range(ap)
            nm = ap.tensor.name
            if nm.startswith("ps"):
                lo, hi = 0, 2048
            m = self.maps.setdefault(nm, IMap())
            deps |= m.access(lo, hi, idx, key, False)
        for ap in writes:
            lo, hi = ap_range(ap)
            nm = ap.tensor.name
            if nm.startswith("ps"):
                lo, hi = 0, 2048
            m = self.maps.setdefault(nm, IMap())
            deps |= m.access(lo, hi, idx, key, True)
        deps.discard(idx)
        op = dict(eng=eng, fn=fn, deps=deps, dkey=dkey, flag=False, val=None, force=set(force))
        deps |= op["force"]
        if eng == "pe":
            self.last_pe = idx
        if dkey is not None:
            n = self.dma_counts.get(dkey, 0) + 1
            self.dma_counts[dkey] = n
            op["val"] = 16 * n
        self.ops.append(op)

    def emit(self, stack):
        nc = self.nc
        ops = self.ops
        for i, op in enumerate(ops):
            for d in op["deps"]:
                o = ops[d]
                if o["dkey"] is None:
                    if o["eng"] == op["eng"] and op["dkey"] is None and d not in op["force"]:
                        if o["eng"] == "pe" or not SAME_SYNC:
                            continue
                    o["flag"] = True
        sems = {}
        for e in ("pe", "act", "dve", "pool"):
            sems[e] = stack.enter_context(nc.semaphore("s_" + e))
        for k in self.dma_counts:
            sems[("d", k)] = stack.enter_context(nc.semaphore("d_" + str(k)))
        cnt = {e: 0 for e in self.ENGS}
        for op in ops:
            if op["dkey"] is None and op["flag"]:
                cnt[op["eng"]] += 1
                op["val"] = cnt[op["eng"]]
        streams = {e: [] for e in self.ENGS}
        for i, op in enumerate(ops):
            streams[op["eng"]].append(i)
        block = stack.enter_context(nc.Block())

        def run(ename, eng):
            waited = {}
            for i in streams[ename]:
                op = ops[i]
                need = {}
                for d in op["deps"]:
                    o = ops[d]
                    if o["dkey"] is None:
                        if o["eng"] == ename and op["dkey"] is None and (ename == "pe" or not SAME_SYNC) \
                                and d not in op["force"]:
                            continue
                        sk = o["eng"]
                    else:
                        sk = ("d", o["dkey"])
                    v = o["val"]
                    if v is None:
                        raise RuntimeError("dep without value")
                    if need.get(sk, 0) < v:
                        need[sk] = v
                for sk, v in need.items():
                    if waited.get(sk, 0) < v:
                        eng.wait_ge(sems[sk], v)
                        waited[sk] = v
                inst = op["fn"](eng)
                if op["dkey"] is not None:
                    inst.then_inc(sems[("d", op["dkey"])], 16)
                elif op["flag"]:
                    inst.then_inc(sems[ename], 1)
            if ename == "sp":
                for k, n in self.dma_counts.items():
                    if str(k).startswith("out"):
                        eng.wait_ge(sems[("d", k)], 16 * n)

        @block.tensor
        def _(e):
            run("pe", e)

        @block.scalar
        def _(e):
            run("act", e)

        @block.vector
        def _(e):
            run("dve", e)

        @block.gpsimd
        def _(e):
            run("pool", e)

        @block.sync
        def _(e):
            run("sp", e)


class WQ:
    def __init__(self, S, WS, pre=3):
        self.S = S
        self.WS = WS
        self.jobs = []
        self.k = 0
        self.issued = 0
        self.pre = pre
        self.layout = [dict(), dict()]
        self.total = [0, 0]
        self.wpk = None

    def _issue(self, j):
        slot = j % NSLOT
        spec = self.jobs[j]
        l, n = spec[1], spec[3] * spec[5]
        off = self.layout[l][spec]
        dst = self.WS[:, slot, 0:n]
        src = self.wpk[l][:, off:off + n]
        self.S.add("sp", (lambda d, s: (lambda e: e.dma_start(out=d.bitcast(F32R), in_=s.bitcast(F32R))))(dst, src),
                   reads=(), writes=(dst,), dkey="w%d" % slot)

    def next(self, spec):
        S = self.S
        if S.collect:
            self.jobs.append(spec)
            l = spec[1]
            if spec not in self.layout[l]:
                self.layout[l][spec] = self.total[l]
                self.total[l] += spec[3] * spec[5]
            return self.WS[:, 0, :]
        k = self.k
        while self.issued < min(len(self.jobs), k + self.pre + 1):
            self._issue(self.issued)
            self.issued += 1
        self.k += 1
        return self.WS[:, k % NSLOT, :]


def pfull(a, b, src):
    return (lambda sl: sl[:, 0:a * b].rearrange("p (a b) -> p a b", a=a, b=b), src)


def build(nc, cfg, stack):
    tiles = cfg["tiles"]
    NL = cfg["layers"]
    stop = cfg.get("stop", None)
    raw_out = cfg.get("raw_out", False)
    skip = cfg.get("skip", ())
    TT = sum(t["W"] for t in tiles)
    TO = sum(t["W"] for t in tiles if not t["halo"])

    xT = nc.dram_tensor("xT", [D, TT], F32, kind="ExternalInput").ap()
    posr = nc.dram_tensor("posr", [P, TT], I32, kind="ExternalInput").ap()
    cstd = nc.dram_tensor("cst", [P, NCST], F32, kind="ExternalInput").ap()
    wstd = nc.dram_tensor("wst", [2, P, 8, 128], F32, kind="ExternalInput").ap()
    identd = nc.dram_tensor("ident", [P, 128], F32, kind="ExternalInput").ap()
    outT = nc.dram_tensor("outT", [D, TO], F32, kind="ExternalOutput").ap()

    def sb(name, shape, dtype):
        return stack.enter_context(nc.sbuf_tensor(name, shape, dtype))

    X = sb("X", [P, DC, 512], F32)
    H = sb("H", [P, DC, 512], F32)
    SCR = sb("SCR", [P, 8192], F32)
    HC = sb("HCB", [P, 4, 544], F32)
    YC = sb("YCB", [P, 4, 512], F32)
    ACTB = sb("ACTB", [P, 8, 512], F32)
    WS = sb("WS", [P, NSLOT, SLOTF], F32)
    CST = sb("CST", [P, NCST], F32)
    TMP = sb("TMP", [P, 3, 512], F32)
    SQB = sb("SQB", [P, 3, 512], F32)
    ONES = sb("ONES", [P, 128], F32)
    ONESB = sb("ONESB", [P, 128], BF16)
    PERMT = sb("PERMT", [P, 128], F32)
    SWAPB = sb("SWAPB", [P, 128], BF16)
    MASKS = sb("MASKS", [P, 4, 128], BF16)
    WST = sb("WST", [P, 2, 8, 128], BF16)
    EXS = sb("EXS", [P, 2, 16], F32)
    KH = sb("KH", [P, 2, 8, 128], BF16)
    VH = sb("VH", [P, 2, 4, 128], BF16)
    HCH = sb("HCH", [P, 2, 4, 32], F32)
    ROPE = sb("ROPE", [P, 2, 512], F32)
    IDENTB = sb("IDENTB", [P, 128], BF16)
    DIAG = sb("DIAG", [P, 3, 128], F32)
    SMALL = sb("SMALL", [P, 16], F32)
    banks = [stack.enter_context(nc.psum_tensor("ps%d" % i, [P, 512], F32)) for i in range(8)]

    S = Sched(nc)
    BANKS7 = [b[:, :] for b in banks[0:7]]
    BANKS5 = BANKS7[0:5]
    CBANK0, CBANK1 = BANKS7[5], BANKS7[6]
    S.banks = BANKS7
    STB = banks[7][:, :]
    wq = WQ(S, WS)

    SCRB = SCR[:, :].bitcast(BF16)
    ACTT = ACTB
    QT = SCRB[:, 0:4096].rearrange("p (a b) -> p a b", a=8, b=512)
    KT = SCRB[:, 4096:8192].rearrange("p (a b) -> p a b", a=8, b=512)
    VV = SCRB[:, 8192:10240].rearrange("p (t g d) -> p t g d", t=4, g=4, d=128)
    UT = SCRB[:, 10240:12288].rearrange("p (a b) -> p a b", a=4, b=512)
    VLN = SCRB[:, 12288:14336].rearrange("p (a b) -> p a b", a=4, b=512)
    R2 = 7168
    QF = SQB[:, 2, :]
    T1 = SCR[:, R2:R2 + 512]
    T2 = SCR[:, R2 + 512:R2 + 1024]
    EB = SCRB[:, 2 * R2:2 * R2 + 2048].rearrange("p (a c b) -> p a c b", a=2, c=2, b=512)
    RF = SCR[:, R2:R2 + 1024].rearrange("p (a b) -> p a b", a=2, b=512)
    PI = SCR[:, R2:R2 + 512].bitcast(I32)

    def R_(ap):
        return ap.bitcast(F32R)

    def mm(out, lhsT, rhs, start, stop, mode="full"):
        force = ()
        if not S.collect and mode != S.pe_mode and "nodrain" not in skip:
            if S.last_pe is not None:
                force = (S.last_pe,)
            S.pe_mode = mode
        S.add("pe", lambda e: e.matmul(out, lhsT, rhs, start=start, stop=stop), reads=(lhsT, rhs), writes=(out,),
              force=force)

    def mmr(out, lhsT, rhs, start, stop):
        mm(out, lhsT.bitcast(F32R), rhs.bitcast(F32R), start, stop)

    def act(out, in_, func, bias=None, scale=None):
        kw = {}
        rd = [in_]
        if bias is not None:
            kw["bias"] = bias
            if not isinstance(bias, (int, float)):
                rd.append(bias)
        if scale is not None:
            kw["scale"] = scale
            if not isinstance(scale, (int, float)):
                rd.append(scale)
        S.add("act", lambda e: e.activation(out, in_, func, **kw), reads=rd, writes=(out,))

    def tt(eng, out, a, b, op):
        S.add(eng, lambda e: e.tensor_tensor(out, a, b, op), reads=(a, b), writes=(out,))

    def ts(eng, out, a, s1, s2, op0, op1=None):
        rd = [a]
        if not isinstance(s1, (int, float)):
            rd.append(s1)
        if s2 is not None and not isinstance(s2, (int, float)):
            rd.append(s2)
        if op1 is None:
            S.add(eng, lambda e: e.tensor_scalar(out, a, s1, None, op0), reads=rd, writes=(out,))
        else:
            S.add(eng, lambda e: e.tensor_scalar(out, a, s1, s2, op0, op1), reads=rd, writes=(out,))

    def stt(out, a, s, b, op0, op1):
        rd = [a, b]
        if not isinstance(s, (int, float)):
            rd.append(s)
        S.add("dve", lambda e: e.scalar_tensor_tensor(out, a, s, b, op0, op1), reads=rd, writes=(out,))

    def cp(eng, out, in_):
        if eng == "act":
            S.add("dve", lambda e: e.tensor_copy(out, in_), reads=(in_,), writes=(out,))
        else:
            S.add(eng, lambda e: e.tensor_copy(out, in_), reads=(in_,), writes=(out,))

    def recip(out, in_):
        S.add("dve", lambda e: e.reciprocal(out, in_), reads=(in_,), writes=(out,))

    def dma(out, in_, dkey, reads=(), writes=()):
        S.add("sp", lambda e: e.dma_start(out=out, in_=in_), reads=reads, writes=writes, dkey=dkey)

    def cs(col, n=1):
        return CST[:, col:col + n]

    have_stats = [False]

    pend = []

    def stat_chunk(c, W, delay=0):
        sq = SQB[:, c % 3, :W]
        act(R_(sq), X[:, c, :W], AF.Square)
        pend.append((c, sq))
        while len(pend) > delay:
            cc, sqq = pend.pop(0)
            mmr(STB[:, :W], ONES[:, :], sqq, cc == 0, cc == DC - 1)
            if cc == DC - 1:
                have_stats[0] = True

    def stat_flush(W):
        while pend:
            cc, sqq = pend.pop(0)
            mmr(STB[:, :W], ONES[:, :], sqq, cc == 0, cc == DC - 1)
            if cc == DC - 1:
                have_stats[0] = True

    def rmsnorm(gcol, W):
        stat_flush(W)
        if not have_stats[0]:
            for c in range(DC):
                stat_chunk(c, W, delay=2)
            stat_flush(W)
        have_stats[0] = False
        RS = STB[:, :W]
        act(RS, STB[:, :W], AF.Sqrt, bias=cs(G_EPS), scale=1.0 / D)
        recip(RS, RS)
        for c in range(DC):
            stt(R_(H[:, c, :W]), X[:, c, :W], cs(gcol + c), RS, ALU.mult, ALU.mult)

    def slab(name, l, col0):
        s = wq.next((name, l, 0, 16, col0, 128))
        return s.rearrange("p (a b) -> p a b", a=16, b=128)

    def ffn(win_n, wout_n, l, gcol, W):
        rmsnorm(gcol, W)
        if stop == "norm":
            dma(outT[:, 0:W].rearrange("(c p) t -> p c t", p=128), H[:, :, :W], "out", reads=(H[:, :, :W],))
            return
        for (f0, f1) in FPARTS:
            nf = f1 - f0
            for i in range(f0, f1):
                sg = slab(win_n, l, i * 128)
                G = S.bank()
                U = S.bank()
                for kc in range(DC):
                    mmr(G[:, :W], sg[:, kc, :], H[:, kc, :W], kc == 0, kc == DC - 1)
                su = slab(win_n, l, FF + i * 128)
                for kc in range(DC):
                    mmr(U[:, :W], su[:, kc, :], H[:, kc, :W], kc == 0, kc == DC - 1)
                a = ACTT[:, i - f0, :W]
                act(R_(a), G[:, :W], AF.Silu)
                tt("dve", R_(a), a, U[:, :W], ALU.mult)
            if stop == "act":
                dma(outT[:, 0:W].rearrange("(c p) t -> p c t", p=128), ACTT[:, :, :W], "out", reads=(ACTT[:, :, :W],))
                return
            for m in range(DC):
                so = wq.next((wout_n, l, f0 * 128, nf, m * 128, 128))
                so = so[:, 0:nf * 128].rearrange("p (a b) -> p a b", a=nf, b=128)
                O = S.bank()
                for k in range(nf):
                    mmr(O[:, :W], so[:, k, :], ACTT[:, k, :W], k == 0, k == nf - 1)
                stt(X[:, m, :W], O[:, :W], 0.5, X[:, m, :W], ALU.mult, ALU.add)
                if f1 == FC:
                    stat_chunk(m, W, delay=2)

    def rope_tables(W):
        PF = TMP[:, 2, :W]
        cp("dve", PF, PI[:, :W])
        for which, fcol, shift in ((0, G_IFC, 0.25), (1, G_IFS, 0.0)):
            Y = TMP[:, 1, :W]
            ts("dve", Y, PF, cs(fcol), shift, ALU.mult, ALU.add)
            KI = PI[:, :W]
            cp("dve", KI, Y)
            KF = TMP[:, 0, :W]
            cp("dve", KF, KI)
            tt("dve", Y, Y, KF, ALU.subtract)
            M = TMP[:, 0, :W]
            ts("dve", M, Y, 0.5, None, ALU.is_gt)
            tt("dve", Y, Y, M, ALU.subtract)
            ts("dve", M, Y, -0.5, None, ALU.is_lt)
            tt("dve", Y, Y, M, ALU.add)
            act(ROPE[:, which, :W], Y, AF.Sin, scale=2.0 * np.pi * (1.0 - 2e-6))

    def proj_fm(l, col0, evac, W, dup64=False):
        s = wq.next(("w_in", l, 0, 16, col0, 128))
        s = s.rearrange("p (a b) -> p a b", a=16, b=128)
        PS = S.bank()
        for kc in range(DC):
            mmr(PS[:, :W], s[:, kc, :], H[:, kc, :W], kc == 0, kc == DC - 1)
        evac(PS)

    def rope_evac(PS, dst, W):
        if "rope" in skip:
            cp("act", dst, PS[:, :W])
            return
        cp("act", R_(QF[:, :W]), PS[:, :W])
        tt("dve", T1[:, :W], PS[:, :W], ROPE[:, 0, :W], ALU.mult)
        QS = S.bank()
        mmr(QS[:, :W], PERMT[:, :], QF[:, :W], True, True)
        tt("dve", T2[:, :W], QS[:, :W], ROPE[:, 1, :W], ALU.mult)
        tt("pool", dst, T1[:, :W], T2[:, :W], ALU.add)

    def conv_gen(l, W):
        LB = l * L_SIZE
        for c0 in (0, 2):
            YPs = [CBANK0, CBANK1]
            for j in range(31):
                for k in range(2):
                    c = c0 + k
                    YP = YPs[k]
                    if j == 0:
                        ts("dve", YP[:, :W], HC[:, c, 0:W], cs(LB + L_CW + c * 31), cs(LB + L_CB + c),
                           ALU.mult, ALU.add)
                    else:
                        stt(YP[:, :W], HC[:, c, j:j + W], cs(LB + L_CW + c * 31 + j), YP[:, :W], ALU.mult, ALU.add)
                    yield
            for k in range(2):
                act(R_(YC[:, c0 + k, :W]), YPs[k][:, :W], AF.Identity)
                yield

    def conv_finish(l, W, CAT):
        LB = l * L_SIZE
        S1 = S.bank()
        S2 = S.bank()
        for c in range(4):
            Y = YC[:, c, :W]
            sq = SQB[:, c % 2, :W]
            act(R_(sq), Y, AF.Square)
            mmr(S1[:, :W], ONES[:, :], Y, c == 0, c == 3)
            mmr(S2[:, :W], ONES[:, :], sq, c == 0, c == 3)
        MEAN = TMP[:, 1, :W]
        TB = TMP[:, 2, :W]
        ts("dve", MEAN, S1[:, :W], 1.0 / 512, None, ALU.mult)
        tt("dve", TB, MEAN, MEAN, ALU.mult)
        stt(TB, S2[:, :W], 1.0 / 512, TB, ALU.mult, ALU.subtract)
        act(TB, TB, AF.Sqrt, bias=cs(G_EPS), scale=1.0)
        recip(TB, TB)
        for c in range(4):
            Y = YC[:, c, :W]
            tt("dve", R_(Y), Y, MEAN, ALU.subtract)
            tt("dve", R_(Y), Y, TB, ALU.mult)
            act(R_(CAT[:, 8 + c, :W]), Y, AF.Silu, bias=cs(LB + L_CLB + c), scale=cs(LB + L_CLG + c))

    def mixer(l, W, halo, full):
        NB = W // 128
        LB = l * L_SIZE
        rmsnorm(LB + L_GM, W)
        if "hist" not in skip:
            cp("pool", R_(HC[:, :, 0:30]), HCH[:, l, :, 0:30])
        for c in range(4 if "glu" not in skip else 0):
            hc = HC[:, c, 30:30 + W]

            def ev_a2(PS, hc=hc):
                act(R_(hc), PS[:, :W], AF.Sigmoid)

            def ev_a1(PS, hc=hc):
                tt("dve", R_(hc), hc, PS[:, :W], ALU.mult)
                if halo:
                    ts("dve", R_(hc), hc, cs(G_VALID), None, ALU.mult)
            proj_fm(l, 2048 + c * 128, ev_a2, W)
            proj_fm(l, 1536 + c * 128, ev_a1, W)
        gen = conv_gen(l, W) if (full and "conv" not in skip) else iter(())
        if full:
            S.banks = BANKS5

        def pump(n):
            for _ in range(n):
                next(gen, None)
        if full and "q" not in skip:
            for c in range(8):
                proj_fm(l, c * 128, lambda PS, c=c: rope_evac(PS, QT[:, c, :W], W), W)
                pump(7)
        for j in range(0 if "k" not in skip else 2, 2):
            ga, gb = 2 * j, 2 * j + 1
            Ea, Oa, Eb, Ob = 2 * ga, 2 * ga + 1, 2 * gb, 2 * gb + 1
            proj_fm(l, 1024 + j * 128, lambda PS, Ea=Ea: rope_evac(PS, KT[:, Ea, :W], W), W)
            KS = S.bank()
            mm(KS[:, :W], SWAPB[:, :], KT[:, Ea, :W], True, True)
            ts("dve", KT[:, Ob, :W], KT[:, Ea, :W], cs(G_MHI), None, ALU.mult)
            ts("dve", KT[:, Ea, :W], KT[:, Ea, :W], cs(G_MLO), None, ALU.mult)
            ts("dve", KT[:, Eb, :W], KS[:, :W], cs(G_MLO), None, ALU.mult)
            ts("dve", KT[:, Oa, :W], KS[:, :W], cs(G_MHI), None, ALU.mult)
            pump(7)
        PVs = [S.bank() for _ in range(NB)]
        for kh in range(2 if "v" not in skip else 0):
            s = wq.next(("w_in", l, kh * 1024, 8, 1280, 256))
            s = s.rearrange("p (a b) -> p a b", a=8, b=256)
            for tb in range(NB):
                for k8 in range(8):
                    kc = kh * 8 + k8
                    mmr(PVs[tb][:, 0:256], H[:, kc, tb * 128:(tb + 1) * 128], s[:, k8, :], kc == 0, kc == DC - 1)
            pump(7)
        for tb in range(NB if "v" not in skip else 0):
            src = PVs[tb][:, 0:256].rearrange("p (g d) -> p g d", g=4, d=64)
            cp("act", VV[:, tb, :, 0:64], src)
            cp("act", VV[:, tb, :, 64:128], src)
        if full and "sguproj" not in skip:
            for c in range(4):
                proj_fm(l, 2560 + c * 128, lambda PS, c=c: cp("act", UT[:, c, :W], PS[:, :W]), W)
                pump(7)
            PVs = [S.bank() for _ in range(NB)]
            for half in range(2):
                for kh in range(2):
                    c0 = 3072 + half * 256
                    s = wq.next(("w_in", l, kh * 1024, 8, c0, 256))
                    s = s.rearrange("p (a b) -> p a b", a=8, b=256)
                    for tb in range(NB):
                        for k8 in range(8):
                            kc = kh * 8 + k8
                            mmr(PVs[tb][:, half * 256:(half + 1) * 256], H[:, kc, tb * 128:(tb + 1) * 128],
                                s[:, k8, :], kc == 0, kc == DC - 1)
                    pump(7)
            for tb in range(NB):
                PV = PVs[tb]
                ST6 = SMALL[:, 0:6]
                MV = SMALL[:, 6:8]
                S.add("dve", lambda e, PV=PV: e.bn_stats(ST6, PV[:, 0:512]), reads=(PV[:, 0:512],), writes=(ST6,))
                S.add("dve", lambda e: e.bn_aggr(MV, ST6), reads=(ST6,), writes=(MV,))
                SD = SMALL[:, 8:9]
                act(SD, SMALL[:, 7:8], AF.Sqrt, bias=cs(G_EPS), scale=1.0)
                recip(SD, SD)
                TA = TMP[:, 2, :]
                ts("dve", TA, PV[:, 0:512], SMALL[:, 6:7], SD, ALU.subtract, ALU.mult)
                tt("pool", TA, TA, CST[:, LB + L_SG:LB + L_SG + 512], ALU.mult)
                tt("pool", VLN[:, tb, :], TA, CST[:, LB + L_SB:LB + L_SB + 512], ALU.add)
        CAT = H
        pump(10000)
        S.banks = BANKS7
        if full and "conv" not in skip:
            conv_finish(l, W, CAT)
        if full and "attn" not in skip:
            it = 0
            for g in range(4):
                for n in range(NB):
                    par = it % 2
                    it += 1
                    noprev = (n == 0 and cur_tile[0] == 0)
                    A = S.bank()
                    B = S.bank()
                    for r in range(4):
                        h = 4 * g + r
                        c = h // 2
                        hp = (h % 2) * 64
                        hs = slice(0, 128)
                        ks = 2 * g + (h % 2)
                        kcur = KT[hs, ks, n * 128:(n + 1) * 128]
                        kprev = KH[hs, l, ks, :] if n == 0 else KT[hs, ks, (n - 1) * 128:n * 128]
                        q = QT[hs, c, n * 128:(n + 1) * 128]
                        md = "full"
                        mm(A[:, r * 128:(r + 1) * 128], kcur, q, True, True, mode=md)
                        if not noprev:
                            mm(B[:, r * 128:(r + 1) * 128], kprev, q, True, True, mode=md)
                    Ec = EB[:, par, 0, :]
                    Ep = EB[:, par, 1, :]
                    if "exppsum" in skip:
                        act(A[:, :], A[:, :], AF.Exp, scale=0.125)
                        tt("dve", Ec.rearrange("p (r t) -> p r t", r=4), A[:, :].rearrange("p (r t) -> p r t", r=4), MASKS[:, 2 if halo else 0, :].unsqueeze(1).broadcast_to([P, 4, 128]), ALU.mult)
                    else:
                        act(Ec, A[:, :], AF.Exp, scale=0.125)
                        tt("pool", Ec.rearrange("p (r t) -> p r t", r=4), Ec.rearrange("p (r t) -> p r t", r=4), MASKS[:, 2 if halo else 0, :].unsqueeze(1).broadcast_to([P, 4, 128]), ALU.mult)
                    if not noprev and "exppsum" not in skip:
                        act(Ep, B[:, :], AF.Exp, scale=0.125)
                    if n > 0:
                        prev_halo = halo
                    else:
                        prev_halo = tiles[cur_tile[0] - 1]["halo"] if cur_tile[0] > 0 else True
                    if not noprev and "exppsum" in skip:
                        act(B[:, :], B[:, :], AF.Exp, scale=0.125)
                        tt("dve", Ep.rearrange("p (r t) -> p r t", r=4), B[:, :].rearrange("p (r t) -> p r t", r=4), MASKS[:, 3 if prev_halo else 1, :].unsqueeze(1).broadcast_to([P, 4, 128]), ALU.mult)
                    elif not noprev:
                        tt("pool", Ep.rearrange("p (r t) -> p r t", r=4), Ep.rearrange("p (r t) -> p r t", r=4), MASKS[:, 3 if prev_halo else 1, :].unsqueeze(1).broadcast_to([P, 4, 128]), ALU.mult)
                    NUM = S.bank()
                    DEN = S.bank()
                    vcur = VV[:, n, g, :]
                    vprev = VH[:, l, g, :] if n == 0 else VV[:, n - 1, g, :]
                    mm(NUM[:, :], vcur, Ec, True, noprev)
                    if not noprev:
                        mm(NUM[:, :], vprev, Ep, False, True)
                    mm(DEN[:, :], ONESB[:, :], Ec, True, noprev)
                    if not noprev:
                        mm(DEN[:, :], ONESB[:, :], Ep, False, True)
                    R = RF[:, par, :]
                    for r in range(4):
                        act(R[:, r * 128:(r + 1) * 128], DEN[:, r * 128:(r + 1) * 128], AF.Identity,
                            bias=EXS[:, l, 4 * g + r:4 * g + r + 1], scale=1.0)
                    recip(R, R)
                    for hpi in range(2):
                        hp = hpi * 64
                        nv = NUM[hp:hp + 64, :].rearrange("p (j i t) -> p j i t", j=2, i=2, t=128)[:, :, hpi, :]
                        rv = R[hp:hp + 64, :].rearrange("p (j i t) -> p j i t", j=2, i=2, t=128)[:, :, hpi, :]
                        tt("dve", R_(CAT[hp:hp + 64, 2 * g:2 * g + 2, n * 128:(n + 1) * 128]), nv, rv, ALU.mult)
        if full and "sgu" not in skip:
            for tb in range(NB):
                ME = S.bank()
                MO = S.bank()
                for i in range(4):
                    lh = VLN[:, tb, i * 128:(i + 1) * 128]
                    mm(ME[:, i * 128:(i + 1) * 128], lh, WST[:, l, 2 * i, :], True, True)
                    mm(MO[:, i * 128:(i + 1) * 128], lh, WST[:, l, 2 * i + 1, :], True, True)
                for hpi, M in ((0, ME), (1, MO)):
                    hp = hpi * 64
                    TA = TMP[hp:hp + 64, 2, :]
                    tt("dve", TA, M[hp:hp + 64, :], CST[hp:hp + 64, LB + L_BST:LB + L_BST + 512], ALU.add)
                    tt("dve", R_(CAT[hp:hp + 64, 12:16, tb * 128:(tb + 1) * 128]),
                       TA.rearrange("p (i t) -> p i t", i=4, t=128), UT[hp:hp + 64, :, tb * 128:(tb + 1) * 128], ALU.mult)
        if stop == "cat":
            dma(outT[:, 0:W].rearrange("(c p) t -> p c t", p=128), CAT[:, :, :W], "out", reads=(CAT[:, :, :W],))
            return
        if "states" not in skip:
            cp("pool", KH[:, l, :, :], KT[:, :, W - 128:W])
            cp("pool", VH[:, l, :, :], VV[:, NB - 1, :, :])
            cp("pool", HCH[:, l, :, 0:30], HC[:, :, W:W + 30])
        if full and "wout" not in skip:
            for m in range(DC):
                so = wq.next(("w_out", l, 0, 16, m * 128, 128))
                so = so.rearrange("p (a b) -> p a b", a=16, b=128)
                O = S.bank()
                for k in range(DC):
                    mmr(O[:, :W], so[:, k, :], CAT[:, k, :W], k == 0, k == DC - 1)
                tt("dve", X[:, m, :W], O[:, :W], X[:, m, :W], ALU.add)
                stat_chunk(m, W, delay=2)

    cur_tile = [0]
    dgi = [0]

    def load_inputs(ti, tcol):
        W = tiles[ti]["W"]
        for c in range(DC):
            dma(X[:, c, :W], xT[c * 128:(c + 1) * 128, tcol:tcol + W], "x%d" % c, writes=(X[:, c, :W],))
        dma(PI[:, :W], posr[:, tcol:tcol + W], "pos", writes=(PI[:, :W],))
        rope_tables(W)

    def program():
        del pend[:]
        have_stats[0] = False
        if not S.collect:
            dma(CST[:, :].bitcast(F32R), cstd[:, :].bitcast(F32R), "cst", writes=(CST[:, :],))
            ts("dve", R_(ONES[:, :]), CST[:, G_MCUR:G_MCUR + 128], 0.0, 1.0, ALU.mult, ALU.add)
            S.add("dve", lambda e: e.memset(ONESB[:, :], 1.0), writes=(ONESB[:, :],))
            cp("dve", R_(PERMT[:, :]), CST[:, G_PERM:G_PERM + 128])
            cp("dve", SWAPB[:, :], CST[:, G_SWAP:G_SWAP + 128])
            dma(SCR[:, 2048:2176], identd[:, :], "idn", writes=(SCR[:, 2048:2176],))
            cp("dve", IDENTB[:, :], SCR[:, 2048:2176])
            S.add("pool", lambda e: e.memset(KH[:, :, :, :], 0.0), writes=(KH[:, :, :, :],))
            S.add("pool", lambda e: e.memset(VH[:, :, :, :], 0.0), writes=(VH[:, :, :, :],))
            S.add("pool", lambda e: e.memset(HCH[:, :, :, :], 0.0), writes=(HCH[:, :, :, :],))
            for kind, col in ((0, G_MCUR), (1, G_MPREV)):
                cp("dve", MASKS[:, kind, :], CST[:, col:col + 128])
                ts("dve", MASKS[:, 2 + kind, :], CST[:, col:col + 128], cs(G_VALID), None, ALU.mult)
            for l in range(NL):
                act(EXS[:, l, :], CST[:, l * L_SIZE + L_SINK:l * L_SIZE + L_SINK + 16], AF.Exp)
                stg = SCR[:, 0:1024].rearrange("p (a b) -> p a b", a=8, b=128)
                dma(stg.bitcast(F32R), wstd[l].bitcast(F32R), "wst", writes=(SCR[:, 0:1024],))
                for hd in range(8):
                    tt("dve", WST[:, l, hd, :], stg[:, hd, :], CST[:, G_MCUR:G_MCUR + 128], ALU.mult)
        tcol = 0
        ocol = 0
        for ti, t in enumerate(tiles):
            cur_tile[0] = ti
            W = t["W"]
            halo = t["halo"]
            if ti == 0:
                load_inputs(0, 0)
            for l in range(NL):
                last_halo = halo and (l == NL - 1)
                ffn("ffn1_w_in", "ffn1_w_out", l, l * L_SIZE + L_G1, W)
                if stop in ("ffn1", "norm", "act"):
                    break
                mixer(l, W, halo, not last_halo)
                if stop in ("mix", "cat"):
                    break
                if not last_halo:
                    ffn("ffn2_w_in", "ffn2_w_out", l, l * L_SIZE + L_G2, W)
            if stop in ("norm", "act", "cat"):
                break
            if not halo and not raw_out:
                rmsnorm(G_FN, W)
            if ti + 1 < len(tiles) and stop is None:
                load_inputs(ti + 1, tcol + W)
            if not halo:
                if raw_out:
                    dma(outT[:, ocol:ocol + W].rearrange("(c p) t -> p c t", p=128), X[:, :, :W], "out", reads=(X[:, :, :W],))
                else:
                    dma(outT[:, ocol:ocol + W].rearrange("(c p) t -> p c t", p=128), H[:, :, :W], "out", reads=(H[:, :, :W],))
                ocol += W
            tcol += W

    S.collect = True
    program()
    S.collect = False
    wq.wpk = [nc.dram_tensor("wpk%d" % l, [P, max(wq.total[l], 1)], F32, kind="ExternalInput").ap() for l in range(2)]
    S.wlayout = (wq.layout, wq.total)
    S.bank_i = 0
    program()
    S.emit(stack)
    return S


def make_consts(inp, valid):
    c = np.zeros((P, NCST), np.float32)

    def chunked(v):
        return np.ascontiguousarray(v.reshape(16, 128).T)

    for l in range(2):
        b = l * L_SIZE
        c[:, b + L_G1:b + L_G1 + 16] = chunked(inp["norm_ffn1"][l])
        c[:, b + L_GM:b + L_GM + 16] = chunked(inp["norm_mix"][l])
        c[:, b + L_G2:b + L_G2 + 16] = chunked(inp["norm_ffn2"][l])
        cw = inp["conv_dw_w"][l]
        c[:, b + L_CW:b + L_CW + 124] = cw.T.reshape(4, 128, 31).transpose(1, 0, 2).reshape(128, 124)
        c[:, b + L_CB:b + L_CB + 4] = inp["conv_dw_b"][l].reshape(4, 128).T
        c[:, b + L_CLG:b + L_CLG + 4] = inp["conv_ln_g"][l].reshape(4, 128).T
        c[:, b + L_CLB:b + L_CLB + 4] = inp["conv_ln_b"][l].reshape(4, 128).T
        c[:, b + L_SINK:b + L_SINK + 16] = inp["attn_sinks"][l][None, :]
        sb_ = inp["sgu_b"][l]
        bst = np.zeros((128, 4, 128), np.float32)
        for i in range(4):
            bst[0:64, i, :] = sb_[2 * i][None, :]
            bst[64:128, i, :] = sb_[2 * i + 1][None, :]
        c[:, b + L_BST:b + L_BST + 512] = bst.reshape(128, 512)
        c[:, b + L_SG:b + L_SG + 512] = inp["sgu_ln_g"][l][None, :]
        c[:, b + L_SB:b + L_SB + 512] = inp["sgu_ln_b"][l][None, :]
    c[:, G_FN:G_FN + 16] = chunked(inp["final_norm"])
    inv_freq = (1.0 / (500000.0 ** (np.arange(0, 16, 2, dtype=np.float32) / np.float32(16)))).astype(np.float32)
    two_pi = np.float32(2.0 * np.pi)
    for p in range(128):
        m = p % 64
        if m < 8:
            c[p, G_IFC] = inv_freq[m] / two_pi
            c[p, G_IFS] = -inv_freq[m] / two_pi
        elif m < 16:
            c[p, G_IFC] = inv_freq[m - 8] / two_pi
            c[p, G_IFS] = inv_freq[m - 8] / two_pi
    c[:, G_VALID] = valid
    kk = np.arange(128)[:, None]
    qq = np.arange(128)[None, :]
    c[:, G_MCUR:G_MCUR + 128] = (kk <= qq).astype(np.float32)
    c[:, G_MPREV:G_MPREV + 128] = (kk > qq).astype(np.float32)
    perm = np.zeros((128, 128), np.float32)
    for m in range(128):
        mm_ = m % 64
        if mm_ < 8:
            perm[m + 8, m] = 1.0
        elif mm_ < 16:
            perm[m - 8, m] = 1.0
    c[:, G_PERM:G_PERM + 128] = perm
    c[:, G_EPS] = EPS
    sw = np.zeros((128, 128), np.float32)
    for m in range(128):
        sw[(m + 64) % 128, m] = 1.0
    c[:, G_SWAP:G_SWAP + 128] = sw
    c[0:64, G_MLO] = 1.0
    c[64:128, G_MHI] = 1.0
    return c


def pack_weights(inputs, layout, total):
    out = []
    for l in range(2):
        a = np.zeros((P, max(total[l], 1)), np.float32)
        for (name, ll, row0, nk, col0, ncols), off in layout[l].items():
            blk = inputs[name][l][row0:row0 + nk * 128, col0:col0 + ncols]
            a[:, off:off + nk * ncols] = blk.reshape(nk, 128, ncols).transpose(1, 0, 2).reshape(128, nk * ncols)
        out.append(a)
    return out


def run(inputs, starts, tiles, layers, stop=None, raw_out=False, core_ids=None, skip=()):
    inputs = {k: np.asarray(v) for k, v in inputs.items()}
    ncores = len(starts)
    cfg = dict(tiles=tiles, layers=layers, stop=stop, raw_out=raw_out, skip=skip)
    nc = bass.Bass("TRN2", target_bir_lowering=False)
    nc.dge_precook = False
    with contextlib.ExitStack() as stack:
        S = build(nc, cfg, stack)
    wpk = pack_weights(inputs, *S.wlayout)
    TT = sum(t["W"] for t in tiles)
    halo_w = sum(t["W"] for t in tiles if t["halo"])
    x = inputs["x"]
    pos = inputs["positions"]
    wst = np.ascontiguousarray(np.transpose(inputs["sgu_w"], (0, 3, 1, 2)))
    in_maps = []
    for (b, s0) in starts:
        lo = s0 - halo_w
        xt = np.zeros((D, TT), np.float32)
        pr = np.zeros((P, TT), np.int32)
        a = max(lo, 0)
        xt[:, a - lo:] = x[b, a:lo + TT, :].T
        pr[:, a - lo:] = pos[b, a:lo + TT][None, :]
        valid = 1.0 if lo >= 0 else 0.0
        in_maps.append({
            "xT": xt, "posr": pr, "cst": make_consts(inputs, valid), "wst": wst,
            "ident": np.eye(128, dtype=np.float32),
            "wpk0": wpk[0], "wpk1": wpk[1],
        })
    res = run_bass_kernel_spmd(nc, in_maps, core_ids=list(range(ncores)))
    return [np.asarray(r["outT"]) for r in res.results]


def kernel(**inputs):
    tiles = [dict(W=256, halo=True)] + [dict(W=512, halo=False) for _ in range(4)]
    starts = [(b, h * 2048) for b in range(4) for h in range(2)]
    outs = run(inputs, starts, tiles, 2)
    out = np.zeros((4, 4096, D), np.float32)
    for (b, s0), o in zip(starts, outs):
        out[b, s0:s0 + 2048, :] = o.T
    return out
```

```python
import contextlib
import numpy as np
import concourse.bass as bass
import concourse.mybir as mybir
from concourse.bass_utils import run_bass_kernel_spmd

dt = mybir.dt
F32, BF16, F32R, I32 = dt.float32, dt.bfloat16, dt.float32r, dt.int32
ALU = mybir.AluOpType
AF = mybir.ActivationFunctionType
ESZ = {F32: 4, F32R: 4, BF16: 2, I32: 4}

P = 128
D = 2048
DC = 16
FF = 5632
FC = 44
FPARTS = [(0, 8), (8, 16), (16, 24), (24, 32), (32, 40), (40, 44)]
INC = 3584
EPS = 1e-5
NSLOT = 4
SLOTF = 2048
SAME_SYNC = True
SOFT_SAME = True

L_G1, L_GM, L_G2 = 0, 16, 32
L_CW = 48
L_CB = L_CW + 124
L_CLG = L_CB + 4
L_CLB = L_CLG + 4
L_SINK = L_CLB + 4
L_BST = L_SINK + 16
L_SG = L_BST + 512
L_SB = L_SG + 512
L_SIZE = L_SB + 512
G_BASE = 2 * L_SIZE
G_FN = G_BASE
G_IFC = G_FN + 16
G_IFS = G_IFC + 1
G_VALID = G_IFS + 1
G_MCUR = G_VALID + 1
G_MPREV = G_MCUR + 128
G_PERM = G_MPREV + 128
G_EPS = G_PERM + 128
G_SWAP = G_EPS + 1
G_MLO = G_SWAP + 128
G_MHI = G_MLO + 1
G_E0 = G_MHI + 1
G_E1 = G_E0 + 1
NCST = G_E1 + 1


def ap_range(ap):
    es = ESZ[ap.dtype]
    pat = ap.ap
    pstride = pat[0][0]
    off = ap.offset % pstride if pstride > 0 else ap.offset
    ext = 1
    for st, cnt in pat[1:]:
        ext += (cnt - 1) * abs(st)
    return off * es, (off + ext) * es


class IMap:
    raw = set()

    def __init__(self):
        self.segs = []

    def access(self, lo, hi, idx, key, write):
        deps = set()
        res = []
        covered = []
        for s in self.segs:
            if s[1] <= lo or s[0] >= hi:
                res.append(s)
                continue
            if s[0] < lo:
                res.append([s[0], lo, s[2], dict(s[3])])
            if s[1] > hi:
                res.append([hi, s[1], s[2], dict(s[3])])
            a, b = max(s[0], lo), min(s[1], hi)
            if s[2] is not None:
                deps.add(s[2])
                if not write:
                    self.raw.add(s[2])
            if write:
                deps.update(s[3].values())
            else:
                r = dict(s[3])
                r[key] = idx
                res.append([a, b, s[2], r])
                covered.append((a, b))
        if write:
            res.append([lo, hi, idx, {}])
        else:
            covered.sort()
            cur = lo
            for a, b in covered:
                if a > cur:
                    res.append([cur, a, None, {key: idx}])
                cur = max(cur, b)
            if cur < hi:
                res.append([cur, hi, None, {key: idx}])
        res.sort(key=lambda s: s[0])
        self.segs = res
        return deps


class Sched:
    ENGS = ("pe", "act", "dve", "pool", "sp")

    def __init__(self, nc):
        self.nc = nc
        self.ops = []
        self.maps = {}
        self.collect = False
        self.banks = []
        self.bank_i = 0
        self.dma_counts = {}
        self.last_pe = None
        self.pe_mode = "full"

    def bank(self):
        b = self.banks[self.bank_i % len(self.banks)]
        self.bank_i += 1
        return b

    def add(self, eng, fn, reads=(), writes=(), dkey=None, force=()):
        if self.collect:
            return
        idx = len(self.ops)
        key = eng if dkey is None else ("d", idx)
        deps = set()
        IMap.raw = set()
        for ap in reads:
            lo, hi = ap_range(ap)
            nm = ap.tensor.name
            if nm.startswith("ps"):
                lo, hi = 0, 2048
            m = self.maps.setdefault(nm, IMap())
            deps |= m.access(lo, hi, idx, key, False)
        for ap in writes:
            lo, hi = ap_range(ap)
            nm = ap.tensor.name
            if nm.startswith("ps"):
                lo, hi = 0, 2048
            m = self.maps.setdefault(nm, IMap())
            deps |= m.access(lo, hi, idx, key, True)
        deps.discard(idx)
        op = dict(eng=eng, fn=fn, deps=deps, dkey=dkey, flag=False, val=None, force=set(force))
        op["soft"] = set(d for d in deps if d not in IMap.raw and self.ops[d]["dkey"] is None
                         and self.ops[d]["eng"] == eng and dkey is None) if SOFT_SAME else set()
        deps |= op["force"]
        if eng == "pe":
            self.last_pe = idx
        if dkey is not None:
            n = self.dma_counts.get(dkey, 0) + 1
            self.dma_counts[dkey] = n
            op["val"] = 16 * n
        self.ops.append(op)

    def emit(self, stack):
        nc = self.nc
        ops = self.ops
        for i, op in enumerate(ops):
            for d in op["deps"]:
                o = ops[d]
                if o["dkey"] is None:
                    if o["eng"] == op["eng"] and op["dkey"] is None and d not in op["force"]:
                        if o["eng"] == "pe" or not SAME_SYNC or d in op["soft"]:
                            continue
                    o["flag"] = True
        sems = {}
        for e in ("pe", "act", "dve", "pool"):
            sems[e] = stack.enter_context(nc.semaphore("s_" + e))
        for k in self.dma_counts:
            sems[("d", k)] = stack.enter_context(nc.semaphore("d_" + str(k)))
        cnt = {e: 0 for e in self.ENGS}
        for op in ops:
            if op["dkey"] is None and op["flag"]:
                cnt[op["eng"]] += 1
                op["val"] = cnt[op["eng"]]
        streams = {e: [] for e in self.ENGS}
        for i, op in enumerate(ops):
            streams[op["eng"]].append(i)
        block = stack.enter_context(nc.Block())

        def run(ename, eng):
            waited = {}
            for i in streams[ename]:
                op = ops[i]
                need = {}
                for d in op["deps"]:
                    o = ops[d]
                    if o["dkey"] is None:
                        if o["eng"] == ename and op["dkey"] is None and d not in op["force"] \
                                and (ename == "pe" or not SAME_SYNC or d in op["soft"]):
                            continue
                        sk = o["eng"]
                    else:
                        sk = ("d", o["dkey"])
                    v = o["val"]
                    if v is None:
                        raise RuntimeError("dep without value")
                    if need.get(sk, 0) < v:
                        need[sk] = v
                for sk, v in need.items():
                    if waited.get(sk, 0) < v:
                        eng.wait_ge(sems[sk], v)
                        waited[sk] = v
                inst = op["fn"](eng)
                if op["dkey"] is not None:
                    inst.then_inc(sems[("d", op["dkey"])], 16)
                elif op["flag"]:
                    inst.then_inc(sems[ename], 1)
            if ename == "sp":
                for k, n in self.dma_counts.items():
                    if str(k).startswith("out"):
                        eng.wait_ge(sems[("d", k)], 16 * n)

        @block.tensor
        def _(e):
            run("pe", e)

        @block.scalar
        def _(e):
            run("act", e)

        @block.vector
        def _(e):
            run("dve", e)

        @block.gpsimd
        def _(e):
            run("pool", e)

        @block.sync
        def _(e):
            run("sp", e)


class WQ:
    def __init__(self, S, WS, pre=3):
        self.S = S
        self.WS = WS
        self.jobs = []
        self.k = 0
        self.issued = 0
        self.pre = pre
        self.layout = [dict(), dict()]
        self.total = [0, 0]
        self.wpk = None

    def _issue(self, j):
        slot = j % NSLOT
        spec = self.jobs[j]
        l, n = spec[1], spec[3] * spec[5]
        off = self.layout[l][spec]
        dst = self.WS[:, slot, 0:n]
        src = self.wpk[l][:, off:off + n]
        self.S.add("sp", (lambda d, s: (lambda e: e.dma_start(out=d.bitcast(F32R), in_=s.bitcast(F32R))))(dst, src),
                   reads=(), writes=(dst,), dkey="w%d" % slot)

    def next(self, spec):
        S = self.S
        if S.collect:
            self.jobs.append(spec)
            l = spec[1]
            if spec not in self.layout[l]:
                self.layout[l][spec] = self.total[l]
                self.total[l] += spec[3] * spec[5]
            return self.WS[:, 0, :]
        k = self.k
        while self.issued < min(len(self.jobs), k + self.pre + 1):
            self._issue(self.issued)
            self.issued += 1
        self.k += 1
        return self.WS[:, k % NSLOT, :]


def pfull(a, b, src):
    return (lambda sl: sl[:, 0:a * b].rearrange("p (a b) -> p a b", a=a, b=b), src)


def build(nc, cfg, stack):
    tiles = cfg["tiles"]
    NL = cfg["layers"]
    stop = cfg.get("stop", None)
    raw_out = cfg.get("raw_out", False)
    skip = cfg.get("skip", ())
    TT = sum(t["W"] for t in tiles)
    TO = sum(t["W"] for t in tiles if not t["halo"])

    xT = nc.dram_tensor("xT", [D, TT], F32, kind="ExternalInput").ap()
    posr = nc.dram_tensor("posr", [P, TT], I32, kind="ExternalInput").ap()
    cstd = nc.dram_tensor("cst", [P, NCST], F32, kind="ExternalInput").ap()
    wstd = nc.dram_tensor("wst", [2, P, 8, 128], F32, kind="ExternalInput").ap()
    outT = nc.dram_tensor("outT", [D, TO], F32, kind="ExternalOutput").ap()

    def sb(name, shape, dtype):
        return stack.enter_context(nc.sbuf_tensor(name, shape, dtype))

    X = sb("X", [P, DC, 512], F32)
    H = sb("H", [P, DC, 512], F32)
    SCR = sb("SCR", [P, 8192], F32)
    HC = sb("HCB", [P, 4, 544], F32)
    YC = sb("YCB", [P, 4, 512], F32)
    ACTB = sb("ACTB", [P, 8, 512], F32)
    WS = sb("WS", [P, NSLOT, SLOTF], F32)
    CST = sb("CST", [P, NCST], F32)
    TMP = sb("TMP", [P, 3, 512], F32)
    SQB = sb("SQB", [P, 3, 512], F32)
    ONES = sb("ONES", [P, 128], F32)
    ONESB = sb("ONESB", [P, 128], BF16)
    PERMT = sb("PERMT", [P, 128], F32)
    SWAPB = sb("SWAPB", [P, 128], BF16)
    MASKS = sb("MASKS", [P, 4, 128], BF16)
    WST = sb("WST", [P, 2, 8, 128], BF16)
    EXS = sb("EXS", [P, 2, 16], F32)
    KH = sb("KH", [P, 2, 8, 128], BF16)
    VH = sb("VH", [P, 2, 4, 128], BF16)
    HCH = sb("HCH", [P, 2, 4, 32], F32)
    ROPE = sb("ROPE", [P, 2, 512], F32)
    SINKB = sb("SINKB", [P, 32], BF16)
    SNKH = sb("SNKH", [P, 32], BF16)
    SNKL = sb("SNKL", [P, 32], BF16)
    SNKF = sb("SNKF", [P, 2, 32], F32)
    SMALL = sb("SMALL", [P, 16], F32)
    banks = [stack.enter_context(nc.psum_tensor("ps%d" % i, [P, 512], F32)) for i in range(8)]

    S = Sched(nc)
    BANKS7 = [b[:, :] for b in banks[0:7]]
    BANKS5 = BANKS7[0:5]
    CBANK0, CBANK1 = BANKS7[5], BANKS7[6]
    S.banks = BANKS7
    STB = banks[7][:, :]
    wq = WQ(S, WS)

    SCRB = SCR[:, :].bitcast(BF16)
    ACTT = ACTB
    QT = SCRB[:, 0:4096].rearrange("p (a b) -> p a b", a=8, b=512)
    KT = SCRB[:, 4096:8192].rearrange("p (a b) -> p a b", a=8, b=512)
    VV = SCRB[:, 8192:10240].rearrange("p (t g d) -> p t g d", t=4, g=4, d=128)
    UT = SCRB[:, 10240:12288].rearrange("p (a b) -> p a b", a=4, b=512)
    VLN = SCRB[:, 12288:14336].rearrange("p (a b) -> p a b", a=4, b=512)
    R2 = 7168
    QF = SQB[:, 2, :]
    T1 = SCR[:, R2:R2 + 512]
    T2 = SCR[:, R2 + 512:R2 + 1024]
    EB = SCRB[:, 2 * R2:2 * R2 + 2048].rearrange("p (a c b) -> p a c b", a=2, c=2, b=512)
    RF = SCR[:, R2:R2 + 1024].rearrange("p (a b) -> p a b", a=2, b=512)
    PI = SCR[:, R2:R2 + 512].bitcast(I32)

    def R_(ap):
        return ap.bitcast(F32R)

    def mm(out, lhsT, rhs, start, stop, mode="full"):
        force = ()
        if not S.collect and mode != S.pe_mode and "nodrain" not in skip:
            if S.last_pe is not None:
                force = (S.last_pe,)
            S.pe_mode = mode
        S.add("pe", lambda e: e.matmul(out, lhsT, rhs, start=start, stop=stop), reads=(lhsT, rhs), writes=(out,),
              force=force)

    def mmr(out, lhsT, rhs, start, stop):
        mm(out, lhsT.bitcast(F32R), rhs.bitcast(F32R), start, stop)

    def act(out, in_, func, bias=None, scale=None):
        kw = {}
        rd = [in_]
        if bias is not None:
            kw["bias"] = bias
            if not isinstance(bias, (int, float)):
                rd.append(bias)
        if scale is not None:
            kw["scale"] = scale
            if not isinstance(scale, (int, float)):
                rd.append(scale)
        S.add("act", lambda e: e.activation(out, in_, func, **kw), reads=rd, writes=(out,))

    def tt(eng, out, a, b, op):
        S.add(eng, lambda e: e.tensor_tensor(out, a, b, op), reads=(a, b), writes=(out,))

    def ts(eng, out, a, s1, s2, op0, op1=None):
        rd = [a]
        if not isinstance(s1, (int, float)):
            rd.append(s1)
        if s2 is not None and not isinstance(s2, (int, float)):
            rd.append(s2)
        if op1 is None:
            S.add(eng, lambda e: e.tensor_scalar(out, a, s1, None, op0), reads=rd, writes=(out,))
        else:
            S.add(eng, lambda e: e.tensor_scalar(out, a, s1, s2, op0, op1), reads=rd, writes=(out,))

    def stt(out, a, s, b, op0, op1):
        rd = [a, b]
        if not isinstance(s, (int, float)):
            rd.append(s)
        S.add("dve", lambda e: e.scalar_tensor_tensor(out, a, s, b, op0, op1), reads=rd, writes=(out,))

    def cp(eng, out, in_):
        if eng == "act":
            S.add("dve", lambda e: e.tensor_copy(out, in_), reads=(in_,), writes=(out,))
        else:
            S.add(eng, lambda e: e.tensor_copy(out, in_), reads=(in_,), writes=(out,))

    def recip(out, in_):
        S.add("dve", lambda e: e.reciprocal(out, in_), reads=(in_,), writes=(out,))

    def dma(out, in_, dkey, reads=(), writes=()):
        S.add("sp", lambda e: e.dma_start(out=out, in_=in_), reads=reads, writes=writes, dkey=dkey)

    def cs(col, n=1):
        return CST[:, col:col + n]

    have_stats = [False]

    pend = []

    def stat_chunk(c, W, delay=0):
        sq = SQB[:, c % 3, :W]
        act(R_(sq), X[:, c, :W], AF.Square)
        pend.append((c, sq))
        while len(pend) > delay:
            cc, sqq = pend.pop(0)
            mmr(STB[:, :W], ONES[:, :], sqq, cc == 0, cc == DC - 1)
            if cc == DC - 1:
                have_stats[0] = True

    def stat_flush(W):
        while pend:
            cc, sqq = pend.pop(0)
            mmr(STB[:, :W], ONES[:, :], sqq, cc == 0, cc == DC - 1)
            if cc == DC - 1:
                have_stats[0] = True

    def rmsnorm(gcol, W):
        stat_flush(W)
        if not have_stats[0]:
            for c in range(DC):
                stat_chunk(c, W, delay=2)
            stat_flush(W)
        have_stats[0] = False
        RS = STB[:, :W]
        act(RS, STB[:, :W], AF.Sqrt, bias=cs(G_EPS), scale=1.0 / D)
        recip(RS, RS)
        for c in range(DC):
            stt(R_(H[:, c, :W]), X[:, c, :W], cs(gcol + c), RS, ALU.mult, ALU.mult)

    def slab(name, l, col0):
        s = wq.next((name, l, 0, 16, col0, 128))
        return s.rearrange("p (a b) -> p a b", a=16, b=128)

    def ffn(win_n, wout_n, l, gcol, W):
        rmsnorm(gcol, W)
        if stop == "norm":
            dma(outT[:, 0:W].rearrange("(c p) t -> p c t", p=128), H[:, :, :W], "out", reads=(H[:, :, :W],))
            return
        for (f0, f1) in FPARTS:
            nf = f1 - f0
            for i in range(f0, f1):
                sg = slab(win_n, l, i * 128)
                G = S.bank()
                U = S.bank()
                for kc in range(DC):
                    mmr(G[:, :W], sg[:, kc, :], H[:, kc, :W], kc == 0, kc == DC - 1)
                su = slab(win_n, l, FF + i * 128)
                for kc in range(DC):
                    mmr(U[:, :W], su[:, kc, :], H[:, kc, :W], kc == 0, kc == DC - 1)
                a = ACTT[:, i - f0, :W]
                act(R_(a), G[:, :W], AF.Silu)
                tt("dve", R_(a), a, U[:, :W], ALU.mult)
            if stop == "act":
                dma(outT[:, 0:W].rearrange("(c p) t -> p c t", p=128), ACTT[:, :, :W], "out", reads=(ACTT[:, :, :W],))
                return
            for m in range(DC):
                so = wq.next((wout_n, l, f0 * 128, nf, m * 128, 128))
                so = so[:, 0:nf * 128].rearrange("p (a b) -> p a b", a=nf, b=128)
                O = S.bank()
                for k in range(nf):
                    mmr(O[:, :W], so[:, k, :], ACTT[:, k, :W], k == 0, k == nf - 1)
                stt(X[:, m, :W], O[:, :W], 0.5, X[:, m, :W], ALU.mult, ALU.add)
                if f1 == FC:
                    stat_chunk(m, W, delay=2)

    def rope_tables(W):
        PF = TMP[:, 2, :W]
        cp("dve", PF, PI[:, :W])
        for which, fcol, shift in ((0, G_IFC, 0.25), (1, G_IFS, 0.0)):
            Y = TMP[:, 1, :W]
            ts("dve", Y, PF, cs(fcol), shift, ALU.mult, ALU.add)
            KI = PI[:, :W]
            cp("dve", KI, Y)
            KF = TMP[:, 0, :W]
            cp("dve", KF, KI)
            tt("dve", Y, Y, KF, ALU.subtract)
            M = TMP[:, 0, :W]
            ts("dve", M, Y, 0.5, None, ALU.is_gt)
            tt("dve", Y, Y, M, ALU.subtract)
            ts("dve", M, Y, -0.5, None, ALU.is_lt)
            tt("dve", Y, Y, M, ALU.add)
            act(ROPE[:, which, :W], Y, AF.Sin, scale=2.0 * np.pi * (1.0 - 2e-6))

    def proj_fm(l, col0, evac, W, dup64=False):
        s = wq.next(("w_in", l, 0, 16, col0, 128))
        s = s.rearrange("p (a b) -> p a b", a=16, b=128)
        PS = S.bank()
        for kc in range(DC):
            mmr(PS[:, :W], s[:, kc, :], H[:, kc, :W], kc == 0, kc == DC - 1)
        evac(PS)

    def rope_evac(PS, dst, W):
        if "rope" in skip:
            cp("act", dst, PS[:, :W])
            return
        cp("act", R_(QF[:, :W]), PS[:, :W])
        tt("dve", T1[:, :W], PS[:, :W], ROPE[:, 0, :W], ALU.mult)
        QS = S.bank()
        mmr(QS[:, :W], PERMT[:, :], QF[:, :W], True, True)
        tt("dve", T2[:, :W], QS[:, :W], ROPE[:, 1, :W], ALU.mult)
        tt("pool", dst, T1[:, :W], T2[:, :W], ALU.add)

    def conv_gen(l, W):
        LB = l * L_SIZE
        for c0 in (0, 2):
            YPs = [CBANK0, CBANK1]
            for j in range(31):
                for k in range(2):
                    c = c0 + k
                    YP = YPs[k]
                    if j == 0:
                        ts("dve", YP[:, :W], HC[:, c, 0:W], cs(LB + L_CW + c * 31), cs(LB + L_CB + c),
                           ALU.mult, ALU.add)
                    else:
                        stt(YP[:, :W], HC[:, c, j:j + W], cs(LB + L_CW + c * 31 + j), YP[:, :W], ALU.mult, ALU.add)
                    yield
            for k in range(2):
                act(R_(YC[:, c0 + k, :W]), YPs[k][:, :W], AF.Identity)
                yield

    def conv_finish(l, W, CAT):
        LB = l * L_SIZE
        S1 = S.bank()
        S2 = S.bank()
        for c in range(4):
            Y = YC[:, c, :W]
            sq = SQB[:, c % 2, :W]
            act(R_(sq), Y, AF.Square)
            mmr(S1[:, :W], ONES[:, :], Y, c == 0, c == 3)
            mmr(S2[:, :W], ONES[:, :], sq, c == 0, c == 3)
        MEAN = TMP[:, 1, :W]
        TB = TMP[:, 2, :W]
        ts("dve", MEAN, S1[:, :W], 1.0 / 512, None, ALU.mult)
        tt("dve", TB, MEAN, MEAN, ALU.mult)
        stt(TB, S2[:, :W], 1.0 / 512, TB, ALU.mult, ALU.subtract)
        act(TB, TB, AF.Sqrt, bias=cs(G_EPS), scale=1.0)
        recip(TB, TB)
        for c in range(4):
            Y = YC[:, c, :W]
            tt("dve", R_(Y), Y, MEAN, ALU.subtract)
            tt("dve", R_(Y), Y, TB, ALU.mult)
            act(R_(CAT[:, 8 + c, :W]), Y, AF.Silu, bias=cs(LB + L_CLB + c), scale=cs(LB + L_CLG + c))

    def mixer(l, W, halo, full):
        NB = W // 128
        LB = l * L_SIZE
        rmsnorm(LB + L_GM, W)
        if "hist" not in skip:
            cp("pool", R_(HC[:, :, 0:30]), HCH[:, l, :, 0:30])
        for c in range(4 if "glu" not in skip else 0):
            hc = HC[:, c, 30:30 + W]

            def ev_a2(PS, hc=hc):
                act(R_(hc), PS[:, :W], AF.Sigmoid)

            def ev_a1(PS, hc=hc):
                tt("dve", R_(hc), hc, PS[:, :W], ALU.mult)
                if halo:
                    ts("dve", R_(hc), hc, cs(G_VALID), None, ALU.mult)
            proj_fm(l, 2048 + c * 128, ev_a2, W)
            proj_fm(l, 1536 + c * 128, ev_a1, W)
        gen = conv_gen(l, W) if (full and "conv" not in skip) else iter(())
        if full:
            S.banks = BANKS5

        def pump(n):
            for _ in range(n):
                next(gen, None)
        if full and "q" not in skip:
            for c in range(8):
                proj_fm(l, c * 128, lambda PS, c=c: rope_evac(PS, QT[:, c, :W], W), W)
                pump(7)
        for j in range(0 if "k" not in skip else 2, 2):
            ga, gb = 2 * j, 2 * j + 1
            Ea, Oa, Eb, Ob = 2 * ga, 2 * ga + 1, 2 * gb, 2 * gb + 1
            proj_fm(l, 1024 + j * 128, lambda PS, Ea=Ea: rope_evac(PS, KT[:, Ea, :W], W), W)
            KS = S.bank()
            mm(KS[:, :W], SWAPB[:, :], KT[:, Ea, :W], True, True)
            ts("dve", KT[:, Ob, :W], KT[:, Ea, :W], cs(G_MHI), None, ALU.mult)
            ts("dve", KT[:, Ea, :W], KT[:, Ea, :W], cs(G_MLO), None, ALU.mult)
            ts("dve", KT[:, Eb, :W], KS[:, :W], cs(G_MLO), None, ALU.mult)
            ts("dve", KT[:, Oa, :W], KS[:, :W], cs(G_MHI), None, ALU.mult)
            pump(7)
        PVs = [S.bank() for _ in range(NB)]
        for kh in range(2 if "v" not in skip else 0):
            s = wq.next(("w_in", l, kh * 1024, 8, 1280, 256))
            s = s.rearrange("p (a b) -> p a b", a=8, b=256)
            for tb in range(NB):
                for k8 in range(8):
                    kc = kh * 8 + k8
                    mmr(PVs[tb][:, 0:256], H[:, kc, tb * 128:(tb + 1) * 128], s[:, k8, :], kc == 0, kc == DC - 1)
            pump(7)
        for tb in range(NB if "v" not in skip else 0):
            src = PVs[tb][:, 0:256].rearrange("p (g d) -> p g d", g=4, d=64)
            cp("act", VV[:, tb, :, 0:64], src)
            cp("act", VV[:, tb, :, 64:128], src)
        if full and "sguproj" not in skip:
            for c in range(4):
                proj_fm(l, 2560 + c * 128, lambda PS, c=c: cp("act", UT[:, c, :W], PS[:, :W]), W)
                pump(7)
            PVs = [S.bank() for _ in range(NB)]
            for half in range(2):
                for kh in range(2):
                    c0 = 3072 + half * 256
                    s = wq.next(("w_in", l, kh * 1024, 8, c0, 256))
                    s = s.rearrange("p (a b) -> p a b", a=8, b=256)
                    for tb in range(NB):
                        for k8 in range(8):
                            kc = kh * 8 + k8
                            mmr(PVs[tb][:, half * 256:(half + 1) * 256], H[:, kc, tb * 128:(tb + 1) * 128],
                                s[:, k8, :], kc == 0, kc == DC - 1)
                    pump(7)
            for tb in range(NB):
                PV = PVs[tb]
                ST6 = SMALL[:, 0:6]
                MV = SMALL[:, 6:8]
                S.add("dve", lambda e, PV=PV: e.bn_stats(ST6, PV[:, 0:512]), reads=(PV[:, 0:512],), writes=(ST6,))
                S.add("dve", lambda e: e.bn_aggr(MV, ST6), reads=(ST6,), writes=(MV,))
                SD = SMALL[:, 8:9]
                act(SD, SMALL[:, 7:8], AF.Sqrt, bias=cs(G_EPS), scale=1.0)
                recip(SD, SD)
                TA = TMP[:, 2, :]
                ts("dve", TA, PV[:, 0:512], SMALL[:, 6:7], SD, ALU.subtract, ALU.mult)
                tt("pool", TA, TA, CST[:, LB + L_SG:LB + L_SG + 512], ALU.mult)
                tt("pool", VLN[:, tb, :], TA, CST[:, LB + L_SB:LB + L_SB + 512], ALU.add)
        CAT = H
        pump(10000)
        S.banks = BANKS7
        if full and "conv" not in skip:
            conv_finish(l, W, CAT)
        if full and "attn" not in skip:
            S.banks = BANKS7 + [STB]
            iters = [(g, n) for g in range(4) for n in range(NB)]

            def bc4(m):
                return m.unsqueeze(1).broadcast_to([P, 4, 128])

            def v4(ap):
                return ap.rearrange("p (r t) -> p r t", r=4)

            def stage1(it, g, n):
                par = it % 2
                noprev = (n == 0 and cur_tile[0] == 0)
                A = S.bank()
                B = S.bank()
                for r in range(4):
                    h = 4 * g + r
                    c = h // 2
                    ks = 2 * g + (h % 2)
                    kcur = KT[:, ks, n * 128:(n + 1) * 128]
                    kprev = KH[:, l, ks, :] if n == 0 else KT[:, ks, (n - 1) * 128:n * 128]
                    q = QT[:, c, n * 128:(n + 1) * 128]
                    mm(A[:, r * 128:(r + 1) * 128], kcur, q, True, True)
                    if not noprev:
                        mm(B[:, r * 128:(r + 1) * 128], kprev, q, True, True)
                Ec = EB[:, par, 0, :]
                Ep = EB[:, par, 1, :]
                act(Ec, A[:, :], AF.Exp, scale=0.125)
                tt("pool", v4(Ec), v4(Ec), bc4(MASKS[:, 2 if halo else 0, :]), ALU.mult)
                if not noprev:
                    act(Ep, B[:, :], AF.Exp, scale=0.125)
                    if n > 0:
                        prev_halo = halo
                    else:
                        prev_halo = tiles[cur_tile[0] - 1]["halo"] if cur_tile[0] > 0 else True
                    tt("dve" if it % 2 == 0 else "pool", v4(Ep), v4(Ep), bc4(MASKS[:, 3 if prev_halo else 1, :]), ALU.mult)

            def stage2(it, g, n):
                par = it % 2
                noprev = (n == 0 and cur_tile[0] == 0)
                Ec = EB[:, par, 0, :]
                Ep = EB[:, par, 1, :]
                NUM = S.bank()
                DEN = S.bank()
                vcur = VV[:, n, g, :]
                vprev = VH[:, l, g, :] if n == 0 else VV[:, n - 1, g, :]
                mm(NUM[:, :], vcur, Ec, True, noprev)
                if not noprev:
                    mm(NUM[:, :], vprev, Ep, False, True)
                mm(DEN[:, :], ONESB[:, :], Ec, True, False)
                if not noprev:
                    mm(DEN[:, :], ONESB[:, :], Ep, False, False)
                snk = SINKB[:, l * 16 + 4 * g:l * 16 + 4 * g + 4].unsqueeze(2).broadcast_to([P, 4, 128])
                mm(v4(DEN[:, :]), ONESB[:, :], snk, False, True)
                R = RF[:, par, :]
                recip(R, DEN[:, :])
                for hpi in range(2):
                    hp = hpi * 64
                    nv = NUM[hp:hp + 64, :].rearrange("p (j i t) -> p j i t", j=2, i=2, t=128)[:, :, hpi, :]
                    rv = R[hp:hp + 64, :].rearrange("p (j i t) -> p j i t", j=2, i=2, t=128)[:, :, hpi, :]
                    tt("dve", R_(CAT[hp:hp + 64, 2 * g:2 * g + 2, n * 128:(n + 1) * 128]), nv, rv, ALU.mult)

            stage1(0, *iters[0])
            for it, (g, n) in enumerate(iters):
                if it + 1 < len(iters):
                    stage1(it + 1, *iters[it + 1])
                stage2(it, g, n)
        S.banks = BANKS7
        if full and "sgu" not in skip:
            for tb in range(NB):
                ME = S.bank()
                MO = S.bank()
                for i in range(4):
                    lh = VLN[:, tb, i * 128:(i + 1) * 128]
                    mm(ME[:, i * 128:(i + 1) * 128], lh, WST[:, l, 2 * i, :], True, True)
                    mm(MO[:, i * 128:(i + 1) * 128], lh, WST[:, l, 2 * i + 1, :], True, True)
                for hpi, M in ((0, ME), (1, MO)):
                    hp = hpi * 64
                    TA = TMP[hp:hp + 64, 2, :]
                    tt("dve", TA, M[hp:hp + 64, :], CST[hp:hp + 64, LB + L_BST:LB + L_BST + 512], ALU.add)
                    tt("dve", R_(CAT[hp:hp + 64, 12:16, tb * 128:(tb + 1) * 128]),
                       TA.rearrange("p (i t) -> p i t", i=4, t=128), UT[hp:hp + 64, :, tb * 128:(tb + 1) * 128], ALU.mult)
        if stop == "cat":
            dma(outT[:, 0:W].rearrange("(c p) t -> p c t", p=128), CAT[:, :, :W], "out", reads=(CAT[:, :, :W],))
            return
        if "states" not in skip:
            cp("pool", KH[:, l, :, :], KT[:, :, W - 128:W])
            cp("pool", VH[:, l, :, :], VV[:, NB - 1, :, :])
            cp("pool", HCH[:, l, :, 0:30], HC[:, :, W:W + 30])
        if full and "wout" not in skip:
            for m in range(DC):
                so = wq.next(("w_out", l, 0, 16, m * 128, 128))
                so = so.rearrange("p (a b) -> p a b", a=16, b=128)
                O = S.bank()
                for k in range(DC):
                    mmr(O[:, :W], so[:, k, :], CAT[:, k, :W], k == 0, k == DC - 1)
                tt("dve", X[:, m, :W], O[:, :W], X[:, m, :W], ALU.add)
                stat_chunk(m, W, delay=2)

    cur_tile = [0]
    dgi = [0]

    def load_inputs(ti, tcol):
        W = tiles[ti]["W"]
        for c in range(DC):
            dma(X[:, c, :W], xT[c * 128:(c + 1) * 128, tcol:tcol + W], "x%d" % c, writes=(X[:, c, :W],))
        dma(PI[:, :W], posr[:, tcol:tcol + W], "pos", writes=(PI[:, :W],))
        rope_tables(W)

    def program():
        del pend[:]
        have_stats[0] = False
        if not S.collect:
            dma(CST[:, :].bitcast(F32R), cstd[:, :].bitcast(F32R), "cst", writes=(CST[:, :],))
            ts("dve", R_(ONES[:, :]), CST[:, G_MCUR:G_MCUR + 128], 0.0, 1.0, ALU.mult, ALU.add)
            S.add("dve", lambda e: e.memset(ONESB[:, :], 1.0), writes=(ONESB[:, :],))
            cp("dve", R_(PERMT[:, :]), CST[:, G_PERM:G_PERM + 128])
            cp("dve", SWAPB[:, :], CST[:, G_SWAP:G_SWAP + 128])

            S.add("pool", lambda e: e.memset(KH[:, :, :, :], 0.0), writes=(KH[:, :, :, :],))
            S.add("pool", lambda e: e.memset(VH[:, :, :, :], 0.0), writes=(VH[:, :, :, :],))
            S.add("pool", lambda e: e.memset(HCH[:, :, :, :], 0.0), writes=(HCH[:, :, :, :],))
            for kind, col in ((0, G_MCUR), (1, G_MPREV)):
                cp("dve", MASKS[:, kind, :], CST[:, col:col + 128])
                ts("dve", MASKS[:, 2 + kind, :], CST[:, col:col + 128], cs(G_VALID), None, ALU.mult)
            for l in range(NL):
                act(EXS[:, l, :], CST[:, l * L_SIZE + L_SINK:l * L_SIZE + L_SINK + 16], AF.Exp)
                sl_ = slice(l * 16, (l + 1) * 16)
                cp("dve", SNKH[:, sl_], EXS[:, l, :])
                cp("dve", SNKF[:, 0, sl_], SNKH[:, sl_])
                tt("dve", SNKF[:, 1, sl_], EXS[:, l, :], SNKF[:, 0, sl_], ALU.subtract)
                cp("dve", SNKL[:, sl_], SNKF[:, 1, sl_])
                ts("dve", SINKB[:, sl_], SNKH[:, sl_], cs(G_E0), None, ALU.mult)
                stt(SINKB[:, sl_], SNKL[:, sl_], cs(G_E1), SINKB[:, sl_], ALU.mult, ALU.add)
                stg = SCR[:, 0:1024].rearrange("p (a b) -> p a b", a=8, b=128)
                dma(stg.bitcast(F32R), wstd[l].bitcast(F32R), "wst", writes=(SCR[:, 0:1024],))
                for hd in range(8):
                    tt("dve", WST[:, l, hd, :], stg[:, hd, :], CST[:, G_MCUR:G_MCUR + 128], ALU.mult)
        tcol = 0
        ocol = 0
        for ti, t in enumerate(tiles):
            cur_tile[0] = ti
            W = t["W"]
            halo = t["halo"]
            if ti == 0:
                load_inputs(0, 0)
            for l in range(NL):
                last_halo = halo and (l == NL - 1)
                ffn("ffn1_w_in", "ffn1_w_out", l, l * L_SIZE + L_G1, W)
                if stop in ("ffn1", "norm", "act"):
                    break
                mixer(l, W, halo, not last_halo)
                if stop in ("mix", "cat"):
                    break
                if not last_halo:
                    ffn("ffn2_w_in", "ffn2_w_out", l, l * L_SIZE + L_G2, W)
            if stop in ("norm", "act", "cat"):
                break
            if not halo and not raw_out:
                rmsnorm(G_FN, W)
            if ti + 1 < len(tiles) and stop is None:
                load_inputs(ti + 1, tcol + W)
            if not halo:
                if raw_out:
                    dma(outT[:, ocol:ocol + W].rearrange("(c p) t -> p c t", p=128), X[:, :, :W], "out", reads=(X[:, :, :W],))
                else:
                    dma(outT[:, ocol:ocol + W].rearrange("(c p) t -> p c t", p=128), H[:, :, :W], "out", reads=(H[:, :, :W],))
                ocol += W
            tcol += W

    S.collect = True
    program()
    S.collect = False
    wq.wpk = [nc.dram_tensor("wpk%d" % l, [P, max(wq.total[l], 1)], F32, kind="ExternalInput").ap() for l in range(2)]
    S.wlayout = (wq.layout, wq.total)
    S.bank_i = 0
    program()
    S.emit(stack)
    return S


def make_consts(inp, valid):
    c = np.zeros((P, NCST), np.float32)

    def chunked(v):
        return np.ascontiguousarray(v.reshape(16, 128).T)

    for l in range(2):
        b = l * L_SIZE
        c[:, b + L_G1:b + L_G1 + 16] = chunked(inp["norm_ffn1"][l])
        c[:, b + L_GM:b + L_GM + 16] = chunked(inp["norm_mix"][l])
        c[:, b + L_G2:b + L_G2 + 16] = chunked(inp["norm_ffn2"][l])
        cw = inp["conv_dw_w"][l]
        c[:, b + L_CW:b + L_CW + 124] = cw.T.reshape(4, 128, 31).transpose(1, 0, 2).reshape(128, 124)
        c[:, b + L_CB:b + L_CB + 4] = inp["conv_dw_b"][l].reshape(4, 128).T
        c[:, b + L_CLG:b + L_CLG + 4] = inp["conv_ln_g"][l].reshape(4, 128).T
        c[:, b + L_CLB:b + L_CLB + 4] = inp["conv_ln_b"][l].reshape(4, 128).T
        c[:, b + L_SINK:b + L_SINK + 16] = inp["attn_sinks"][l][None, :]
        sb_ = inp["sgu_b"][l]
        bst = np.zeros((128, 4, 128), np.float32)
        for i in range(4):
            bst[0:64, i, :] = sb_[2 * i][None, :]
            bst[64:128, i, :] = sb_[2 * i + 1][None, :]
        c[:, b + L_BST:b + L_BST + 512] = bst.reshape(128, 512)
        c[:, b + L_SG:b + L_SG + 512] = inp["sgu_ln_g"][l][None, :]
        c[:, b + L_SB:b + L_SB + 512] = inp["sgu_ln_b"][l][None, :]
    c[:, G_FN:G_FN + 16] = chunked(inp["final_norm"])
    inv_freq = (1.0 / (500000.0 ** (np.arange(0, 16, 2, dtype=np.float32) / np.float32(16)))).astype(np.float32)
    two_pi = np.float32(2.0 * np.pi)
    for p in range(128):
        m = p % 64
        if m < 8:
            c[p, G_IFC] = inv_freq[m] / two_pi
            c[p, G_IFS] = -inv_freq[m] / two_pi
        elif m < 16:
            c[p, G_IFC] = inv_freq[m - 8] / two_pi
            c[p, G_IFS] = inv_freq[m - 8] / two_pi
    c[:, G_VALID] = valid
    kk = np.arange(128)[:, None]
    qq = np.arange(128)[None, :]
    c[:, G_MCUR:G_MCUR + 128] = (kk <= qq).astype(np.float32)
    c[:, G_MPREV:G_MPREV + 128] = (kk > qq).astype(np.float32)
    perm = np.zeros((128, 128), np.float32)
    for m in range(128):
        mm_ = m % 64
        if mm_ < 8:
            perm[m + 8, m] = 1.0
        elif mm_ < 16:
            perm[m - 8, m] = 1.0
    c[:, G_PERM:G_PERM + 128] = perm
    c[:, G_EPS] = EPS
    sw = np.zeros((128, 128), np.float32)
    for m in range(128):
        sw[(m + 64) % 128, m] = 1.0
    c[:, G_SWAP:G_SWAP + 128] = sw
    c[0:64, G_MLO] = 1.0
    c[64:128, G_MHI] = 1.0
    c[0, G_E0] = 1.0
    c[1, G_E1] = 1.0
    return c


def pack_weights(inputs, layout, total):
    out = []
    for l in range(2):
        a = np.zeros((P, max(total[l], 1)), np.float32)
        for (name, ll, row0, nk, col0, ncols), off in layout[l].items():
            blk = inputs[name][l][row0:row0 + nk * 128, col0:col0 + ncols]
            a[:, off:off + nk * ncols] = blk.reshape(nk, 128, ncols).transpose(1, 0, 2).reshape(128, nk * ncols)
        out.append(a)
    return out


def run(inputs, starts, tiles, layers, stop=None, raw_out=False, core_ids=None, skip=()):
    inputs = {k: np.asarray(v) for k, v in inputs.items()}
    ncores = len(starts)
    cfg = dict(tiles=tiles, layers=layers, stop=stop, raw_out=raw_out, skip=skip)
    nc = bass.Bass("TRN2", target_bir_lowering=False)
    nc.dge_precook = False
    with contextlib.ExitStack() as stack:
        S = build(nc, cfg, stack)
    wpk = pack_weights(inputs, *S.wlayout)
    TT = sum(t["W"] for t in tiles)
    halo_w = sum(t["W"] for t in tiles if t["halo"])
    x = inputs["x"]
    pos = inputs["positions"]
    wst = np.ascontiguousarray(np.transpose(inputs["sgu_w"], (0, 3, 1, 2)))
    in_maps = []
    for (b, s0) in starts:
        lo = s0 - halo_w
        xt = np.zeros((D, TT), np.float32)
        pr = np.zeros((P, TT), np.int32)
        a = max(lo, 0)
        xt[:, a - lo:] = x[b, a:lo + TT, :].T
        pr[:, a - lo:] = pos[b, a:lo + TT][None, :]
        valid = 1.0 if lo >= 0 else 0.0
        in_maps.append({
            "xT": xt, "posr": pr, "cst": make_consts(inputs, valid), "wst": wst,
            "wpk0": wpk[0], "wpk1": wpk[1],
        })
    res = run_bass_kernel_spmd(nc, in_maps, core_ids=list(range(ncores)))
    return [np.asarray(r["outT"]) for r in res.results]


def kernel(**inputs):
    tiles = [dict(W=256, halo=True)] + [dict(W=512, halo=False) for _ in range(4)]
    starts = [(b, h * 2048) for b in range(4) for h in range(2)]
    outs = run(inputs, starts, tiles, 2)
    out = np.zeros((4, 4096, D), np.float32)
    for (b, s0), o in zip(starts, outs):
        out[b, s0:s0 + 2048, :] = o.T
    return out
```
